# Optimizing a Trainium2 kernel written in Bass

```python
import jax, jax.numpy as jnp
from jax import lax
import numpy as np

D_MODEL = 4096
BATCH = 8
SEQ = 2048
DEPTH = 4

D_MIX = D_MODEL
HEAD_DIM = 128
W_CONV = D_MIX // 2
W_LRU = D_MIX - W_CONV
H_CONV = W_CONV // HEAD_DIM
H_LRU = W_LRU // HEAD_DIM
D_IN = 3 * W_CONV + 2 * W_LRU
K_CONV = 3
K_LRU = 4
LRU_C = 8.0
D_FF = -(-(8 * D_MODEL) // (3 * 256)) * 256
EPS = 1e-6

kernel_name = "hybrid_shortconv_rglru_swiglu"


def rms_norm(x, g):
    xf = x.astype(jnp.float32)
    y = xf * lax.rsqrt(jnp.mean(xf * xf, axis=-1, keepdims=True) + EPS)
    return (y * g.astype(jnp.float32)).astype(x.dtype)


def head_rms_norm(y, g, n_heads):
    b, s, w = y.shape
    yf = y.astype(jnp.float32).reshape(b, s, n_heads, w // n_heads)
    yf = yf * lax.rsqrt(jnp.mean(yf * yf, axis=-1, keepdims=True) + EPS)
    return (yf.reshape(b, s, w) * g.astype(jnp.float32)).astype(y.dtype)


def causal_depthwise_conv(x, w):
    k_width = w.shape[0]
    s = x.shape[1]
    xp = jnp.pad(x, ((0, 0), (k_width - 1, 0), (0, 0)))
    return sum(xp[:, k:k + s, :] * w[k] for k in range(k_width))


def rg_lru(xr, w_gate_r, b_gate_r, w_gate_i, b_gate_i, lru_lambda):
    b, s, w = xr.shape
    xh = xr.reshape(b, s, H_LRU, HEAD_DIM)
    r = jax.nn.sigmoid(jnp.einsum('bshd,hde->bshe', xh, w_gate_r).reshape(b, s, w) + b_gate_r)
    i = jax.nn.sigmoid(jnp.einsum('bshd,hde->bshe', xh, w_gate_i).reshape(b, s, w) + b_gate_i)
    log_a = -LRU_C * r.astype(jnp.float32) * jax.nn.softplus(-lru_lambda.astype(jnp.float32))
    a = jnp.exp(log_a)
    mult = jnp.sqrt(-jnp.expm1(2.0 * log_a))
    is_first = (jnp.arange(s) == 0)[None, :, None]
    mult = jnp.where(is_first, 1.0, mult)
    u = mult * (i * xr).astype(jnp.float32)

    def combine(c1, c2):
        a1, b1 = c1
        a2, b2 = c2
        return a1 * a2, a2 * b1 + b2

    _, h = lax.associative_scan(combine, (a, u), axis=1)
    return h.astype(xr.dtype)


def hybrid_mixer(u, w_in, conv_a, conv_b, conv_b_bias, w_gate_r, b_gate_r, w_gate_i, b_gate_i,
                 lru_lambda, out_norm_a, out_norm_b, w_out):
    proj = jnp.einsum('bsd,de->bse', u, w_in)
    g_b, g_c, x_a, x_r, g_y = jnp.split(
        proj, [W_CONV, 2 * W_CONV, 3 * W_CONV, 3 * W_CONV + W_LRU], axis=-1)
    y_a = g_b * causal_depthwise_conv(g_c * x_a, conv_a)
    x_r = causal_depthwise_conv(x_r, conv_b) + conv_b_bias
    y_b = rg_lru(x_r, w_gate_r, b_gate_r, w_gate_i, b_gate_i, lru_lambda) * jax.nn.gelu(g_y)
    y = jnp.concatenate([head_rms_norm(y_a, out_norm_a, H_CONV),
                         head_rms_norm(y_b, out_norm_b, H_LRU)], axis=-1)
    return jnp.einsum('bse,ed->bsd', y, w_out)


def swiglu_ffn(u, w_gate, w_up, w_down):
    h = jax.nn.silu(jnp.einsum('bsd,df->bsf', u, w_gate)) * jnp.einsum('bsd,df->bsf', u, w_up)
    return jnp.einsum('bsf,fd->bsd', h, w_down)


def setup_inputs(seed: int = 0) -> dict:
    key = jax.random.key(seed)
    ks = jax.random.split(key, 20)
    f32 = jnp.float32

    def nrm(k, shape, fan_in):
        return jax.random.normal(k, shape, f32) * (fan_in ** -0.5)

    def gain(k, shape):
        return 1.0 + 0.02 * jax.random.normal(k, shape, f32)

    def bias(k, shape):
        return 0.02 * jax.random.normal(k, shape, f32)

    a0 = jax.random.uniform(ks[10], (DEPTH, W_LRU), f32, 0.9, 0.999)
    s = a0 ** (1.0 / LRU_C)
    lru_lambda = jnp.log(s) - jnp.log1p(-s)

    return {
        "x": jax.random.normal(ks[0], (BATCH, SEQ, D_MODEL), f32),
        "norm_mix": gain(ks[1], (DEPTH, D_MODEL)),
        "w_in": nrm(ks[2], (DEPTH, D_MODEL, D_IN), D_MODEL),
        "conv_a": nrm(ks[3], (DEPTH, K_CONV, W_CONV), K_CONV),
        "conv_b": nrm(ks[4], (DEPTH, K_LRU, W_LRU), K_LRU),
        "conv_b_bias": bias(ks[5], (DEPTH, W_LRU)),
        "w_gate_r": nrm(ks[6], (DEPTH, H_LRU, HEAD_DIM, HEAD_DIM), HEAD_DIM),
        "b_gate_r": bias(ks[7], (DEPTH, W_LRU)),
        "w_gate_i": nrm(ks[8], (DEPTH, H_LRU, HEAD_DIM, HEAD_DIM), HEAD_DIM),
        "b_gate_i": bias(ks[9], (DEPTH, W_LRU)),
        "lru_lambda": lru_lambda,
        "out_norm_a": gain(ks[11], (DEPTH, W_CONV)),
        "out_norm_b": gain(ks[12], (DEPTH, W_LRU)),
        "w_out": nrm(ks[13], (DEPTH, D_MIX, D_MODEL), D_MIX),
        "norm_ffn": gain(ks[14], (DEPTH, D_MODEL)),
        "w_ffn_gate": nrm(ks[15], (DEPTH, D_MODEL, D_FF), D_MODEL),
        "w_ffn_up": nrm(ks[16], (DEPTH, D_MODEL, D_FF), D_MODEL),
        "w_ffn_down": nrm(ks[17], (DEPTH, D_FF, D_MODEL), D_FF),
        "final_norm": gain(ks[18], (D_MODEL,)),
    }


def reference(x, norm_mix, w_in, conv_a, conv_b, conv_b_bias, w_gate_r, b_gate_r, w_gate_i,
              b_gate_i, lru_lambda, out_norm_a, out_norm_b, w_out, norm_ffn, w_ffn_gate,
              w_ffn_up, w_ffn_down, final_norm):
    h = x
    for l in range(DEPTH):
        u = rms_norm(h, norm_mix[l])
        h = h + hybrid_mixer(u, w_in[l], conv_a[l], conv_b[l], conv_b_bias[l], w_gate_r[l],
                             b_gate_r[l], w_gate_i[l], b_gate_i[l], lru_lambda[l],
                             out_norm_a[l], out_norm_b[l], w_out[l])
        u = rms_norm(h, norm_ffn[l])
        h = h + swiglu_ffn(u, w_ffn_gate[l], w_ffn_up[l], w_ffn_down[l])
    return rms_norm(h, final_norm)
```

```python
import numpy as np
from contextlib import ExitStack
from functools import partial

import concourse.bass as bass
import concourse.mybir as mybir
from concourse.bass_utils import run_bass_kernel_spmd

F32 = mybir.dt.float32
BF16 = mybir.dt.bfloat16
AF = mybir.ActivationFunctionType
ALU = mybir.AluOpType

EPS = 1e-6
LRU_C = 8.0
GELU_TANH = True


class Cfg:
    def __init__(self, D=4096, S=2048, DEPTH=4, F=11008, TT=1024, KC=16, NSLOT=4, NFG=3, IDLE_US=30.0):
        self.IDLE_US = IDLE_US
        self.D, self.S, self.DEPTH, self.F, self.TT, self.KC, self.NSLOT = D, S, DEPTH, F, TT, KC, NSLOT
        self.WC = D // 2
        self.WL = D - self.WC
        self.NDC = D // 128
        self.NHC = self.WC // 128
        self.NHL = self.WL // 128
        self.DIN = 3 * self.WC + 2 * self.WL
        self.NFC = F // 128
        assert F % 128 == 0 and D % 128 == 0 and S % TT == 0 and TT % 512 == 0
        self.NP = S // TT
        self.NH = TT // 512
        base = self.NFC // NFG
        rem = self.NFC - base * NFG
        self.FGS = [base + (1 if i < rem else 0) for i in range(NFG)]
        self.FGMAX = max(self.FGS)
        self.YCH = max(self.NDC, self.FGMAX)
        o = 0
        self.P_G1 = o; o += self.NDC
        self.P_G2 = o; o += self.NDC
        self.P_CA = o; o += 3 * self.NHC
        self.P_CB = o; o += 4 * self.NHL
        self.P_CBB = o; o += self.NHL
        self.P_BR = o; o += self.NHL
        self.P_BI = o; o += self.NHL
        self.P_LAM = o; o += self.NHL
        self.P_GNA = o; o += self.NHC
        self.P_GNB = o; o += self.NHL
        self.NPRM = o


class Op:
    __slots__ = ("eng", "fn", "deps", "marked", "num", "dma")


class DmaEv:
    __slots__ = ("sem", "val")

    def __init__(self, sem, val):
        self.sem, self.val = sem, val


class Buf:
    __slots__ = ("name", "w", "r")

    def __init__(self, name):
        self.name, self.w, self.r = name, [], {}


def handoff(old, new):
    evs = {}
    for b in old:
        for e in b.w:
            evs[id(e)] = e
        for e in b.r.values():
            evs[id(e)] = e
    for n in new:
        n.w = list(evs.values())
        n.r = {}


class Prog:
    ENGS = ("pe", "act", "dve", "pool", "sp")

    def __init__(self):
        self.ops = {e: [] for e in self.ENGS}
        self.dma_cnt = {}

    def _collect(self, eng, reads, writes):
        deps = {}
        for b in reads:
            for e in b.w:
                deps[id(e)] = e
        for b in writes:
            for e in b.w:
                deps[id(e)] = e
            for e in b.r.values():
                deps[id(e)] = e
        out = []
        for d in deps.values():
            if isinstance(d, Op):
                if eng == "pe" and d.eng == "pe" and d.dma is None:
                    continue
                d.marked = True
            out.append(d)
        return out

    def compute(self, eng, fn, reads=(), writes=()):
        op = Op()
        op.eng, op.fn, op.marked, op.num, op.dma = eng, fn, False, 0, None
        op.deps = self._collect(eng, reads, writes)
        for b in reads:
            b.r[eng] = op
        for b in writes:
            b.w = [op]
            b.r = {}
        self.ops[eng].append(op)
        return op

    def dma(self, eng, fn, sem, reads=(), writes=()):
        op = Op()
        op.eng, op.fn, op.marked, op.num = eng, fn, False, 0
        self.dma_cnt[sem] = self.dma_cnt.get(sem, 0) + 16
        ev = DmaEv(sem, self.dma_cnt[sem])
        op.dma = ev
        op.deps = self._collect(eng, reads, writes)
        for b in reads:
            b.r[("dma", sem)] = ev
        for b in writes:
            b.w = [ev]
            b.r = {}
        self.ops[eng].append(op)
        return ev

    def replay(self, eng, e, engsem, final_waits=()):
        ops = self.ops[eng]
        known = {}
        for op in ops:
            waits = {}
            for d in op.deps:
                if isinstance(d, DmaEv):
                    s, v = d.sem, d.val
                else:
                    s, v = engsem[d.eng], d.num
                k = id(s)
                if k not in waits or waits[k][1] < v:
                    waits[k] = (s, v)
            for k, (s, v) in waits.items():
                if known.get(k, 0) < v:
                    e.wait_ge(s, v)
                    known[k] = v
            inst = op.fn(e)
            if op.marked:
                inst.then_inc(engsem[eng], 1)
            if op.dma is not None:
                inst.then_inc(op.dma.sem, 16)
        for (s, v) in final_waits:
            e.wait_ge(s, v)

    def number(self):
        for eng in self.ENGS:
            c = 0
            for op in self.ops[eng]:
                if op.marked and op.dma is None:
                    c += 1
                    op.num = c
                elif op.marked and op.dma is not None:
                    raise AssertionError


def build_program(cfg, debug_h=False):
    c = cfg
    D, S, TT, NDC, NHC, NHL, NH, NP, KC = c.D, c.S, c.TT, c.NDC, c.NHC, c.NHL, c.NH, c.NP, c.KC
    assert NHC == NHL
    nc = bass.Bass("TRN2", target_bir_lowering=False)

    def din(name, shape):
        return nc.dram_tensor(name, list(shape), F32, kind="ExternalInput").ap()

    x = din("x", [S, D])
    w_in = din("w_in", [c.DEPTH, D, c.DIN])
    w_gr = din("w_gate_r", [c.DEPTH, NHL, 128, 128])
    w_gi = din("w_gate_i", [c.DEPTH, NHL, 128, 128])
    w_out = din("w_out", [c.DEPTH, D, D])
    w_fg = din("w_ffn_gate", [c.DEPTH, D, c.F])
    w_fu = din("w_ffn_up", [c.DEPTH, D, c.F])
    w_fd = din("w_ffn_down", [c.DEPTH, c.F, D])
    prm_d = din("prm", [128, c.DEPTH, c.NPRM])
    fng_d = din("fng", [128, NDC])
    ident_d = din("ident", [128, 128])
    out = nc.dram_tensor("out", [S, D], F32, kind="ExternalOutput").ap()
    hres = nc.dram_tensor("hres", [D, S], F32).ap()
    HV = hres.rearrange("(c p) s -> p c s", p=128)

    P = Prog()
    es = ExitStack()
    with es:
        def newsem(name):
            return es.enter_context(nc.semaphore(name))

        TW = TT + 4
        U_W = NDC * TT // 2
        Y_W = c.YCH * TT // 2
        WS_W = KC * 128 // 2
        GW_W = 2 * 128 // 2
        small_w = (c.DEPTH * c.NPRM + c.DEPTH * NHL + 3 * NHL + c.DEPTH * (2 * NHC + 3 * NHL + NHL)
                   + 64 + 128 + NDC + 8 + 2 * GW_W)
        budget_w = (206 * 1024) // 4
        NT = (budget_w - U_W - Y_W - c.NSLOT * WS_W - small_w) // TW
        NT = min(NT, 14)
        assert NT >= 13, NT
        AW = U_W + Y_W + c.NSLOT * WS_W + small_w + NT * TW
        arena = es.enter_context(nc.sbuf_tensor("arena", [128, AW], F32))
        off = [0]

        def carve(n):
            a = arena[:, off[0]:off[0] + n]
            off[0] += n
            return a

        U_raw = carve(U_W)
        Y_raw = carve(Y_W)
        Ubf = U_raw.bitcast(BF16).rearrange("p (c t) -> p c t", t=TT)
        Ybf = Y_raw.bitcast(BF16).rearrange("p (c t) -> p c t", t=TT)
        WS = [carve(WS_W).bitcast(BF16).rearrange("p (k n) -> p k n", n=128) for _ in range(c.NSLOT)]
        GWS = [carve(GW_W).bitcast(BF16).rearrange("p (k n) -> p k n", n=128) for _ in range(2)]
        PRMA = carve(c.DEPTH * c.NPRM).rearrange("p (l n) -> p l n", n=c.NPRM)
        SCA = carve(c.DEPTH * NHL).rearrange("p (l n) -> p l n", n=NHL)
        SCW = carve(3 * NHL)
        CXTA = carve(c.DEPTH * NHC * 2).rearrange("p (l j k) -> p l j k", j=NHC, k=2)
        XRTA = carve(c.DEPTH * NHL * 3).rearrange("p (l j k) -> p l j k", j=NHL, k=3)
        HSTA = carve(c.DEPTH * NHL).rearrange("p (l j) -> p l j", j=NHL)
        ONES = carve(64).bitcast(BF16)
        IDENT = carve(128)
        FNG = carve(NDC)
        EPSC = carve(8)
        Tt = [carve(TW) for _ in range(NT)]
        assert off[0] == AW

        bU = [Buf(f"U{i}") for i in range(NDC)]
        bY = [Buf(f"Y{i}") for i in range(c.YCH)]
        bWS = [Buf(f"WS{i}") for i in range(c.NSLOT)]
        bGW = [[Buf(f"GW{i}r"), Buf(f"GW{i}i")] for i in range(2)]
        bPRM, bSC, bSCW, bONES, bIDENT, bFNG, bEPS = (Buf(n) for n in ("PRM", "SC", "SCW", "ONES", "IDENT", "FNG", "EPS"))
        bCXT = [[Buf(f"CXT{l}_{j}") for j in range(NHC)] for l in range(c.DEPTH)]
        bXRT = [[Buf(f"XRT{l}_{j}") for j in range(NHL)] for l in range(c.DEPTH)]
        bHST = [[Buf(f"HST{l}_{j}") for j in range(NHL)] for l in range(c.DEPTH)]
        bT = [Buf(f"T{i}") for i in range(NT)]
        bH = [[Buf(f"H{ci}_{p}") for p in range(NP)] for ci in range(NDC)]

        T_R, T_SQ, T_SG0, T_SG1 = 0, 1, 2, 3
        HL_T = [NT - 3, NT - 2, NT - 1]
        R = Tt[T_R][:, 0:TT]
        bR = bT[T_R]

        pm = es.enter_context(nc.psum_tensor("pm", [128, 2 * NH, 512], F32))
        pa = es.enter_context(nc.psum_tensor("pa", [128, 4, 512], F32))
        pmflat = pm[:].rearrange("p b n -> p (b n)")
        paflat = pa[:].rearrange("p b n -> p (b n)")
        bPM = [[Buf(f"PM{s}_{h}") for h in range(NH)] for s in range(2)]
        bPA = [Buf(f"PA{k}") for k in range(4)]

        def pm_slot(s):
            return pmflat[:, s * TT:(s + 1) * TT]

        def psb(s):
            return [bPM[s][h] for h in range(NH)]

        def pa_pair(k):
            return paflat[:, k * 512:(k + NH) * 512]

        engsem = {e: newsem(f"s_{e}") for e in ("pe", "act", "dve", "pool")}
        engsem["sp"] = None
        sem_ws = [newsem(f"s_ws{i}") for i in range(c.NSLOT)]
        sem_gw = [[newsem(f"s_gw{i}r"), newsem(f"s_gw{i}i")] for i in range(2)]

        def act(out_, in_, func, reads, writes, bias=None, scale=None):
            kw = {}
            if bias is not None:
                kw["bias"] = bias
            if scale is not None:
                kw["scale"] = scale
            return P.compute("act", lambda e: e.activation(out=out_, in_=in_, func=func, **kw), reads, writes)

        def tt(out_, in0, in1, op, reads, writes):
            return P.compute("dve", lambda e: e.tensor_tensor(out=out_, in0=in0, in1=in1, op=op), reads, writes)

        def ts(out_, in0, s1, s2, op0, op1, reads, writes):
            return P.compute("dve", lambda e: e.tensor_scalar(out=out_, in0=in0, scalar1=s1, scalar2=s2, op0=op0, op1=op1),
                             reads, writes)

        def stt(out_, in0, scalar, in1, op0, op1, reads, writes):
            return P.compute("dve", lambda e: e.scalar_tensor_tensor(out=out_, in0=in0, scalar=scalar, in1=in1, op0=op0, op1=op1),
                             reads, writes)

        def dcopy(out_, in_, reads, writes):
            return P.compute("dve", lambda e: e.tensor_copy(out=out_, in_=in_), reads, writes)

        def dmemset(ap, val, writes):
            return P.compute("dve", lambda e: e.memset(ap, val), (), writes)

        wt_list = []
        wt_state = {"issued": 0, "next_use": 0}

        def plan_weight(dram_ap_fn, K_chunks):
            tiles = []
            k0 = 0
            while k0 < K_chunks:
                kc = min(KC, K_chunks - k0)
                wt_list.append(dram_ap_fn(k0, kc))
                tiles.append((len(wt_list) - 1, k0, kc))
                k0 += kc
            return tiles

        def ensure_weights(upto):
            upto = min(upto, len(wt_list) - 1)
            while wt_state["issued"] <= upto:
                i = wt_state["issued"]
                s = i % c.NSLOT
                src = wt_list[i]
                kc = src.shape[1]
                dst = WS[s][:, 0:kc, :]
                P.dma("pool", lambda e, dst=dst, src=src: e.dma_start(out=dst, in_=src), sem_ws[s], (), [bWS[s]])
                wt_state["issued"] += 1

        def chunk_job(tiles, rhs_fn, rhs_bufs, slot):
            K = sum(t[2] for t in tiles)
            for (ti, k0, kc) in tiles:
                assert ti == wt_state["next_use"], (ti, wt_state["next_use"])
                wt_state["next_use"] += 1
                ensure_weights(ti + c.NSLOT - 1)
                s = ti % c.NSLOT
                for kk in range(kc):
                    k = k0 + kk
                    for h in range(NH):
                        lhsT = WS[s][:, kk, :]
                        rhs = rhs_fn(k, h)
                        o = pm[:, slot * NH + h, :]
                        P.compute("pe", lambda e, o=o, lhsT=lhsT, rhs=rhs, st=(k == 0), sp=(k == K - 1):
                                  e.matmul(o, lhsT, rhs, start=st, stop=sp),
                                  [bWS[s], rhs_bufs[k]], [bPM[slot][h]])

        aux_state = {"next": 0}

        def aux_alloc(n):
            k = aux_state["next"]
            if k + n > 4:
                k = 0
            aux_state["next"] = (k + n) % 4
            return k

        slot_ctr = [0]

        def next_slot():
            s = slot_ctr[0] % 2
            slot_ctr[0] += 1
            return s

        def pe_idle():
            if c.IDLE_US > 0:
                cyc = int(c.IDLE_US * 1200)
                P.compute("pe", lambda e: e.nop(cycle_cnt=cyc), (), ())

        rhsU = lambda k, h: Ubf[:, k, h * 512:(h + 1) * 512]
        rhsY = lambda k, h: Ybf[:, k, h * 512:(h + 1) * 512]

        P.dma("sp", lambda e: e.dma_start(out=IDENT, in_=ident_d), newsem("s_c0"), (), [bIDENT])
        P.dma("sp", lambda e: e.dma_start(out=FNG, in_=fng_d), newsem("s_c1"), (), [bFNG])
        P.dma("sp", lambda e: e.dma_start(out=PRMA, in_=prm_d), newsem("s_c2"), (), [bPRM])
        dmemset(ONES, 1.0, [bONES])
        dmemset(EPSC, EPS, [bEPS])
        for l in range(c.DEPTH):
            SC = SCA[:, l, :]
            lam = PRMA[:, l, c.P_LAM:c.P_LAM + NHL]
            act(SC, lam, AF.Exp, [bPRM], [bSC], scale=-1.0)
            ZS, Z2, PS = SCW[:, 0:NHL], SCW[:, NHL:2 * NHL], SCW[:, 2 * NHL:3 * NHL]
            dv = lambda f: P.compute("dve", f, [bSC, bSCW], [bSCW])
            dv(lambda e, SC=SC: e.tensor_scalar(out=ZS, in0=SC, scalar1=2.0, scalar2=None, op0=ALU.add))
            dv(lambda e: e.reciprocal(out=ZS, in_=ZS))
            dv(lambda e, SC=SC: e.tensor_tensor(out=ZS, in0=ZS, in1=SC, op=ALU.mult))
            dv(lambda e: e.tensor_tensor(out=Z2, in0=ZS, in1=ZS, op=ALU.mult))
            dv(lambda e: e.tensor_scalar(out=PS, in0=Z2, scalar1=1.0 / 9.0, scalar2=1.0 / 7.0, op0=ALU.mult, op1=ALU.add))
            for cf in (1.0 / 5.0, 1.0 / 3.0, 1.0):
                dv(lambda e: e.tensor_tensor(out=PS, in0=PS, in1=Z2, op=ALU.mult))
                dv(lambda e, cf=cf: e.tensor_scalar(out=PS, in0=PS, scalar1=cf, scalar2=None, op0=ALU.add))
            P.compute("dve", lambda e, SC=SC: e.scalar_tensor_tensor(out=SC, in0=ZS, scalar=-2.0 * LRU_C, in1=PS, op0=ALU.mult, op1=ALU.mult),
                      [bSCW], [bSC, bSCW])

        sem_hl = [newsem(f"s_hl{i}") for i in range(3)]
        sem_hs = [newsem(f"s_hs{i}") for i in range(3)]
        hl_state = {"n": 0}

        def h_load(ci, p):
            i = hl_state["n"] % 3
            hl_state["n"] += 1
            t = HL_T[i]
            dst = Tt[t][:, 0:TT]
            src = HV[:, ci, p * TT:(p + 1) * TT]
            P.dma("sp", lambda e, dst=dst, src=src: e.dma_start(out=dst, in_=src), sem_hl[i], [bH[ci][p]], [bT[t]])
            return i, t

        def h_store(i, t, ci, p):
            src = Tt[t][:, 0:TT]
            dst = HV[:, ci, p * TT:(p + 1) * TT]
            P.dma("sp", lambda e, dst=dst, src=src: e.dma_start(out=dst, in_=src), sem_hs[i], [bT[t]], [bH[ci][p]])

        def rstd_from_stats(k, dst_t, dst_b, nparts):
            act(dst_t, pa_pair(k), AF.Sqrt, [bPA[k + h] for h in range(NH)] + [bEPS], [dst_b], bias=EPSC[:, 0:1], scale=1.0 / nparts)
            P.compute("dve", lambda e: e.reciprocal(out=dst_t, in_=dst_t), [dst_b], [dst_b])

        def sq_ring():
            SQ = Tt[T_SQ][:, 0:TT].bitcast(BF16).rearrange("p (s t) -> p s t", t=TT)
            sqb = [Buf("SQa"), Buf("SQb")]
            handoff([bT[T_SQ]], sqb)
            return SQ, sqb

        def stats_mm(k, SQ, sqb, sl, first, last):
            for h in range(NH):
                o = pa[:, k + h, :]
                rhs = SQ[:, sl, h * 512:(h + 1) * 512]
                P.compute("pe", lambda e, o=o, rhs=rhs: e.matmul(o, ONES, rhs, start=first, stop=last),
                          [bONES, sqb[sl]], [bPA[k + h]])

        def write_u(ci, H, hb, gcol):
            ts(Ubf[:, ci, :], H, gcol, None, ALU.mult, ALU.bypass, [hb, bPRM], [bU[ci]])

        def norm_standalone(p, gcol_fn):
            k = aux_alloc(NH)
            SQ, sqb = sq_ring()
            for ci in range(NDC):
                i, t = h_load(ci, p)
                H = Tt[t][:, 0:TT]
                sl = ci % 2
                act(SQ[:, sl, :], H, AF.Square, [bT[t]], [sqb[sl]])
                if gcol_fn is not None:
                    write_u(ci, H, bT[t], gcol_fn(ci))
                stats_mm(k, SQ, sqb, sl, ci == 0, ci == NDC - 1)
            rstd_from_stats(k, R, bR, D)
            handoff(sqb, [bT[T_SQ]])

        def resid_phase(jobs, rhs_fn, rhs_bufs, p, nxt):
            if nxt is not None:
                k = aux_alloc(NH)
                SQ, sqb = sq_ring()
            pend = None
            for ci in range(NDC):
                s = next_slot()
                chunk_job(jobs[ci], rhs_fn, rhs_bufs, s)
                if pend is not None:
                    pend()
                    pend = None
                i, t = h_load(ci, p)
                H = Tt[t][:, 0:TT]
                tt(H, pm_slot(s), H, ALU.add, psb(s) + [bT[t]], [bT[t]])
                h_store(i, t, ci, p)
                if nxt is not None:
                    sl = ci % 2
                    act(SQ[:, sl, :], H, AF.Square, [bT[t]], [sqb[sl]])
                    if nxt["g"] is not None:
                        write_u(ci, H, bT[t], nxt["g"](ci))
                    pend = partial(stats_mm, k, SQ, sqb, sl, ci == 0, ci == NDC - 1)
            if pend is not None:
                pend()
            if nxt is not None:
                rstd_from_stats(k, R, bR, D)
                handoff(sqb, [bT[T_SQ]])

        def wv(w2d):
            return w2d.rearrange("(k p) n -> p k n", p=128)

        def reg(w2d_view, col0, K_chunks, krow0=0):
            return plan_weight(lambda k0, kc: w2d_view[:, krow0 + k0:krow0 + k0 + kc, col0:col0 + 128], K_chunks)

        plan = []
        for p in range(NP):
            for l in range(c.DEPTH):
                Wi, Wo, Wg, Wu, Wd = wv(w_in[l]), wv(w_out[l]), wv(w_fg[l]), wv(w_fu[l]), wv(w_fd[l])
                m1 = []
                for j in range(NHL):
                    m1.append({"B": reg(Wi, j * 128, NDC), "C": reg(Wi, (NHC + j) * 128, NDC),
                               "XA": reg(Wi, (2 * NHC + j) * 128, NDC), "XR": reg(Wi, (3 * NHC + j) * 128, NDC),
                               "GY": reg(Wi, (3 * NHC + NHL + j) * 128, NDC)})
                m2 = [reg(Wo, ci * 128, NDC) for ci in range(NDC)]
                ff = []
                f0 = 0
                for fg in c.FGS:
                    f1 = [(reg(Wg, (f0 + fi) * 128, NDC), reg(Wu, (f0 + fi) * 128, NDC)) for fi in range(fg)]
                    f2 = [reg(Wd, ci * 128, fg, krow0=f0) for ci in range(NDC)]
                    ff.append((f1, f2, fg))
                    f0 += fg
                plan.append((m1, m2, ff))

        XW = D
        assert 4 * XW <= U_W + Y_W and 1024 + 4 * TW <= U_W + Y_W
        XIN = [arena[:, i * XW:(i + 1) * XW] for i in range(2)]
        XT = [arena[:, (2 + i) * XW:(3 + i) * XW] for i in range(2)]
        bXIN = [Buf("XIN0"), Buf("XIN1")]
        bXT = [Buf("XT0"), Buf("XT1")]
        sem_xin = [newsem("s_xin0"), newsem("s_xin1")]
        sem_xt = [newsem("s_xt0"), newsem("s_xt1")]
        OS = [arena[:, i * 512:(i + 1) * 512] for i in range(2)]
        ONT = [arena[:, 1024 + i * TW:1024 + (i + 1) * TW] for i in range(4)]
        bOS = [Buf("OS0"), Buf("OS1")]
        bONT = [Buf(f"ONT{i}") for i in range(4)]
        sem_os = [newsem("s_os0"), newsem("s_os1")]
        os_n = [0]
        gw_n = [0]

        pi = 0
        for p in range(NP):
            first = (p == 0)
            last = (p == NP - 1)
            handoff(bU + bY, bXIN + bXT)
            bHX = []
            for t in range(p * TT // 128, (p + 1) * TT // 128):
                s = t % 2
                src = x[t * 128:(t + 1) * 128, :]
                P.dma("sp", lambda e, d=XIN[s], src=src: e.dma_start(out=d, in_=src), sem_xin[s], (), [bXIN[s]])
                for g in range(0, NDC, 4):
                    k = aux_alloc(1)
                    ng = min(4, NDC - g)
                    for q in range(ng):
                        ci = g + q
                        o = pa[:, k, q * 128:(q + 1) * 128]
                        i_ = XIN[s][:, ci * 128:(ci + 1) * 128]
                        P.compute("pe", lambda e, o=o, i_=i_: e.transpose(o, i_, IDENT), [bXIN[s], bIDENT], [bPA[k]])
                    dst = XT[s][:, g * 128:(g + ng) * 128]
                    srcp = pa[:, k, 0:ng * 128]
                    if (g // 4) % 2 == 0:
                        act(dst, srcp, AF.Copy, [bPA[k]], [bXT[s]])
                    else:
                        dcopy(dst, srcp, [bPA[k]], [bXT[s]])
                dd = HV[:, :, t * 128:(t + 1) * 128]
                ss = XT[s].rearrange("p (c t) -> p c t", t=128)
                hb = Buf(f"HX{t}")
                bHX.append(hb)
                P.dma("sp", lambda e, dd=dd, ss=ss: e.dma_start(out=dd, in_=ss), sem_xt[s], [bXT[s]], [hb])
            handoff(bHX, [bH[ci][p] for ci in range(NDC)])
            handoff(bXIN + bXT, bU + bY)

            norm_standalone(p, lambda ci: PRMA[:, 0, c.P_G1 + ci:c.P_G1 + ci + 1])

            for l in range(c.DEPTH):
                m1, m2, ff = plan[pi]
                pi += 1
                PR = PRMA[:, l, :]
                SC = SCA[:, l, :]
                CXT, XRT, HST = CXTA[:, l], XRTA[:, l], HSTA[:, l]
                gcolf = lambda base, l_: (lambda ci: PRMA[:, l_, base + ci:base + ci + 1])

                mt = [(Tt[i], bT[i]) for i in range(1, 13)]
                (Bs, bBs), (Cs, bCs), (CXH, bCXH), (ACC, bACC), (XRH, bXRH), (XC, bXC), (RG, bRG), (IG, bIG), \
                    (AG, bAG), (GS, bGS), (HB2, bHB2), (HF, bHF) = mt
                SQA = HB2[:, 0:TT // 2].bitcast(BF16)
                SQB = HB2[:, TT // 2:TT].bitcast(BF16)
                XCB = HF[:, 0:TT // 2].bitcast(BF16)
                bSQA, bSQB, bXCB = Buf("SQA"), Buf("SQB"), Buf("XCB")
                handoff([bHB2], [bSQA, bSQB])
                handoff([bHF], [bXCB])

                pending = []
                jobs_done = [0]

                def run_due(force=False):
                    while pending and (force or pending[0][0] <= jobs_done[0]):
                        pending.pop(0)[1]()

                def main_job(tiles):
                    s = next_slot()
                    chunk_job(tiles, rhsU, bU, s)
                    jobs_done[0] += 1
                    return s

                for j in range(NHL):
                    hd = m1[j]
                    gs_ = gw_n[0] % 2
                    gw_n[0] += 1
                    for gi_, wsrc in enumerate((w_gr[l, j], w_gi[l, j])):
                        P.dma("pool", lambda e, d=GWS[gs_][:, gi_, :], wsrc=wsrc: e.dma_start(out=d, in_=wsrc),
                              sem_gw[gs_][gi_], (), [bGW[gs_][gi_]])
                    sB = main_job(hd["B"]); run_due()
                    sC = main_job(hd["C"]); run_due()
                    tt(Bs[:, 0:TT], pm_slot(sB), R, ALU.mult, psb(sB) + [bR], [bBs])
                    tt(Cs[:, 0:TT], pm_slot(sC), R, ALU.mult, psb(sC) + [bR], [bCs])
                    tt(Cs[:, 0:TT], Cs[:, 0:TT], R, ALU.mult, [bCs, bR], [bCs])
                    if first:
                        dmemset(CXH[:, 0:2], 0.0, [bCXH])
                    else:
                        dcopy(CXH[:, 0:2], CXT[:, j, :], [bCXT[l][j]], [bCXH])
                    sX = main_job(hd["XA"]); run_due()
                    tt(CXH[:, 2:2 + TT], pm_slot(sX), Cs[:, 0:TT], ALU.mult, psb(sX) + [bCs], [bCXH])
                    if not last:
                        dcopy(CXT[:, j, :], CXH[:, TT:TT + 2], [bCXH], [bCXT[l][j]])
                    ca = lambda k, j=j: PR[:, c.P_CA + k * NHC + j:c.P_CA + k * NHC + j + 1]
                    ts(ACC[:, 0:TT], CXH[:, 0:TT], ca(0), None, ALU.mult, ALU.bypass, [bCXH, bPRM], [bACC])
                    stt(ACC[:, 0:TT], CXH[:, 1:TT + 1], ca(1), ACC[:, 0:TT], ALU.mult, ALU.add, [bCXH, bPRM, bACC], [bACC])
                    stt(ACC[:, 0:TT], CXH[:, 2:TT + 2], ca(2), ACC[:, 0:TT], ALU.mult, ALU.add, [bCXH, bPRM, bACC], [bACC])
                    tt(Cs[:, 0:TT], ACC[:, 0:TT], Bs[:, 0:TT], ALU.mult, [bACC, bBs], [bCs])
                    act(SQA, Cs[:, 0:TT], AF.Square, [bCs], [bSQA])

                    def statsA(j=j):
                        k = aux_alloc(NH)
                        for h in range(NH):
                            o = pa[:, k + h, :]
                            rhs = SQA[:, h * 512:(h + 1) * 512]
                            P.compute("pe", lambda e, o=o, rhs=rhs: e.matmul(o, ONES, rhs, start=True, stop=True),
                                      [bONES, bSQA], [bPA[k + h]])
                        rstd_from_stats(k, Bs[:, 0:TT], bBs, 128)
                        stt(Ybf[:, j, :], Cs[:, 0:TT], PR[:, c.P_GNA + j:c.P_GNA + j + 1], Bs[:, 0:TT], ALU.mult, ALU.mult,
                            [bCs, bPRM, bBs], [bY[j]])
                    pending.append((jobs_done[0] + 2, statsA))

                    sR = main_job(hd["XR"]); run_due()
                    if first:
                        dmemset(XRH[:, 0:3], 0.0, [bXRH])
                    else:
                        dcopy(XRH[:, 0:3], XRT[:, j, :], [bXRT[l][j]], [bXRH])
                    tt(XRH[:, 3:3 + TT], pm_slot(sR), R, ALU.mult, psb(sR) + [bR], [bXRH])
                    if not last:
                        dcopy(XRT[:, j, :], XRH[:, TT:TT + 3], [bXRH], [bXRT[l][j]])
                    cb = lambda k, j=j: PR[:, c.P_CB + k * NHL + j:c.P_CB + k * NHL + j + 1]
                    pcol = lambda base, j=j: PR[:, base + j:base + j + 1]
                    ts(XC[:, 0:TT], XRH[:, 0:TT], cb(0), pcol(c.P_CBB), ALU.mult, ALU.add, [bXRH, bPRM], [bXC])
                    for k in range(1, 4):
                        stt(XC[:, 0:TT], XRH[:, k:TT + k], cb(k), XC[:, 0:TT], ALU.mult, ALU.add, [bXRH, bPRM, bXC], [bXC])
                    act(XCB, XC[:, 0:TT], AF.Copy, [bXC], [bXCB])
                    sG = main_job(hd["GY"]); run_due()
                    gf = AF.Gelu_apprx_tanh if GELU_TANH else AF.Gelu
                    tt(GS[:, 0:TT], pm_slot(sG), R, ALU.mult, psb(sG) + [bR], [bGS])
                    act(GS[:, 0:TT], GS[:, 0:TT], gf, [bGS], [bGS])

                    def gates(j=j, pcol=pcol, gs_=gs_):
                        for (gi_, dstT, dstB, bcol) in ((0, RG, bRG, c.P_BR), (1, IG, bIG, c.P_BI)):
                            k = aux_alloc(NH)
                            for h in range(NH):
                                o = pa[:, k + h, :]
                                rhs = XCB[:, h * 512:(h + 1) * 512]
                                lhsT = GWS[gs_][:, gi_, :]
                                P.compute("pe", lambda e, o=o, lhsT=lhsT, rhs=rhs: e.matmul(o, lhsT, rhs, start=True, stop=True),
                                          [bGW[gs_][gi_], bXCB], [bPA[k + h]])
                            act(dstT[:, 0:TT], pa_pair(k), AF.Sigmoid, [bPA[k + h] for h in range(NH)] + [bPRM], [dstB],
                                bias=pcol(bcol))
                        act(AG[:, 0:TT], RG[:, 0:TT], AF.Exp, [bRG, bSC], [bAG], scale=SC[:, j:j + 1])
                        tt(RG[:, 0:TT], AG[:, 0:TT], AG[:, 0:TT], ALU.mult, [bAG], [bRG])
                        act(RG[:, 0:TT], RG[:, 0:TT], AF.Sqrt, [bRG], [bRG], bias=1.0, scale=-1.0)
                        if first:
                            dmemset(RG[:, 0:1], 1.0, [bRG])
                        tt(IG[:, 0:TT], IG[:, 0:TT], XC[:, 0:TT], ALU.mult, [bIG, bXC], [bIG])
                        tt(IG[:, 0:TT], IG[:, 0:TT], RG[:, 0:TT], ALU.mult, [bIG, bRG], [bIG])
                        init = 0.0 if first else HST[:, j:j + 1]
                        rd = [bAG, bIG] + ([] if first else [bHST[l][j]])
                        P.compute("dve", lambda e, init=init: e.tensor_tensor_scan(out=XC[:, 0:TT], data0=AG[:, 0:TT], data1=IG[:, 0:TT],
                                                                                 initial=init, op0=ALU.mult, op1=ALU.add),
                                  rd, [bXC])
                        if not last:
                            dcopy(HST[:, j:j + 1], XC[:, TT - 1:TT], [bXC], [bHST[l][j]])
                        tt(XC[:, 0:TT], XC[:, 0:TT], GS[:, 0:TT], ALU.mult, [bXC, bGS], [bXC])
                        act(SQB, XC[:, 0:TT], AF.Square, [bXC], [bSQB])

                        def statsB(j=j):
                            k = aux_alloc(NH)
                            for h in range(NH):
                                o = pa[:, k + h, :]
                                rhs = SQB[:, h * 512:(h + 1) * 512]
                                P.compute("pe", lambda e, o=o, rhs=rhs: e.matmul(o, ONES, rhs, start=True, stop=True),
                                          [bONES, bSQB], [bPA[k + h]])
                            rstd_from_stats(k, IG[:, 0:TT], bIG, 128)
                            stt(Ybf[:, NHC + j, :], XC[:, 0:TT], PR[:, c.P_GNB + j:c.P_GNB + j + 1], IG[:, 0:TT], ALU.mult, ALU.mult,
                                [bXC, bPRM, bIG], [bY[NHC + j]])
                        pending.append((jobs_done[0] + 2, statsB))
                    pending.append((jobs_done[0] + 1, gates))
                while pending:
                    run_due(force=True)
                handoff([bSQA, bSQB], [bHB2])
                handoff([bXCB], [bHF])

                pe_idle()
                resid_phase(m2, rhsY, bY, p, {"g": gcolf(c.P_G2, l)})

                pe_idle()
                for gi, (f1, f2, fg) in enumerate(ff):
                    SG = [(Tt[T_SG0], bT[T_SG0]), (Tt[T_SG1], bT[T_SG1])]
                    for fi in range(fg):
                        sg_ = next_slot()
                        chunk_job(f1[fi][0], rhsU, bU, sg_)
                        su_ = next_slot()
                        chunk_job(f1[fi][1], rhsU, bU, su_)
                        T_, bT_ = SG[fi % 2]
                        tt(T_[:, 0:TT], pm_slot(sg_), R, ALU.mult, psb(sg_) + [bR], [bT_])
                        act(T_[:, 0:TT], T_[:, 0:TT], AF.Silu, [bT_], [bT_])
                        tt(T_[:, 0:TT], T_[:, 0:TT], R, ALU.mult, [bT_, bR], [bT_])
                        tt(Ybf[:, fi, :], pm_slot(su_), T_[:, 0:TT], ALU.mult, psb(su_) + [bT_], [bY[fi]])
                    if gi < len(ff) - 1:
                        nxt = None
                    elif l < c.DEPTH - 1:
                        nxt = {"g": gcolf(c.P_G1, l + 1)}
                    else:
                        nxt = {"g": None}
                    resid_phase(f2, rhsY, bY, p, nxt)
                    pe_idle()

            handoff(bU + bY, bOS + bONT)
            for g in range(0, NDC, 4):
                ng = min(4, NDC - g)
                for q in range(ng):
                    ci = g + q
                    i, t = h_load(ci, p)
                    stt(ONT[q][:, 0:TT], Tt[t][:, 0:TT], FNG[:, ci:ci + 1], R, ALU.mult, ALU.mult, [bT[t], bFNG, bR], [bONT[q]])
                for tk in range(TT // 128):
                    ka = aux_alloc(1)
                    for q in range(ng):
                        o = pa[:, ka, q * 128:(q + 1) * 128]
                        i_ = ONT[q][:, tk * 128:(tk + 1) * 128]
                        P.compute("pe", lambda e, o=o, i_=i_: e.transpose(o, i_, IDENT), [bONT[q], bIDENT], [bPA[ka]])
                    so = os_n[0] % 2
                    os_n[0] += 1
                    dst = OS[so][:, 0:ng * 128]
                    if so == 0:
                        act(dst, pa[:, ka, 0:ng * 128], AF.Copy, [bPA[ka]], [bOS[so]])
                    else:
                        dcopy(dst, pa[:, ka, 0:ng * 128], [bPA[ka]], [bOS[so]])
                    r0 = p * TT + tk * 128
                    od = out[r0:r0 + 128, g * 128:(g + ng) * 128]
                    P.dma("sp", lambda e, od=od, dst=dst: e.dma_start(out=od, in_=dst), sem_os[so], [bOS[so]], [Buf("o")])
            handoff(bOS + bONT, bU + bY)

        P.number()
        finals = [(s, v) for s, v in ((sm, P.dma_cnt.get(sm, 0)) for sm in (sem_os + sem_hs)) if v > 0]
        block = es.enter_context(nc.Block())

        @block.tensor
        def _(e):
            P.replay("pe", e, engsem)

        @block.scalar
        def _(e):
            P.replay("act", e, engsem)

        @block.vector
        def _(e):
            P.replay("dve", e, engsem)

        @block.gpsimd
        def _(e):
            P.replay("pool", e, engsem)

        @block.sync
        def _(e):
            P.replay("sp", e, engsem, final_waits=finals)

    stats = {e: len(P.ops[e]) for e in Prog.ENGS}
    return nc, stats


def pack_params(cfg, inp):
    c = cfg
    prm = np.zeros((c.DEPTH, 128, c.NPRM), np.float32)

    def cols(v, n):
        return np.ascontiguousarray(np.asarray(v, np.float32).reshape(c.DEPTH, n, 128).transpose(0, 2, 1))

    prm[:, :, c.P_G1:c.P_G1 + c.NDC] = cols(inp["norm_mix"], c.NDC)
    prm[:, :, c.P_G2:c.P_G2 + c.NDC] = cols(inp["norm_ffn"], c.NDC)
    ca = np.asarray(inp["conv_a"], np.float32)
    for k in range(3):
        prm[:, :, c.P_CA + k * c.NHC:c.P_CA + (k + 1) * c.NHC] = cols(ca[:, k, :], c.NHC)
    cb = np.asarray(inp["conv_b"], np.float32)
    for k in range(4):
        prm[:, :, c.P_CB + k * c.NHL:c.P_CB + (k + 1) * c.NHL] = cols(cb[:, k, :], c.NHL)
    prm[:, :, c.P_CBB:c.P_CBB + c.NHL] = cols(inp["conv_b_bias"], c.NHL)
    prm[:, :, c.P_BR:c.P_BR + c.NHL] = cols(inp["b_gate_r"], c.NHL)
    prm[:, :, c.P_BI:c.P_BI + c.NHL] = cols(inp["b_gate_i"], c.NHL)
    prm[:, :, c.P_LAM:c.P_LAM + c.NHL] = cols(inp["lru_lambda"], c.NHL)
    prm[:, :, c.P_GNA:c.P_GNA + c.NHC] = cols(inp["out_norm_a"], c.NHC)
    prm[:, :, c.P_GNB:c.P_GNB + c.NHL] = cols(inp["out_norm_b"], c.NHL)
    fng = np.ascontiguousarray(np.asarray(inp["final_norm"], np.float32).reshape(c.NDC, 128).T)
    prm = np.ascontiguousarray(prm.transpose(1, 0, 2))
    return prm, fng


def run(cfg, inp, n_cores, debug_h=False, trace=False):
    nc, stats = build_program(cfg, debug_h=debug_h)
    prm, fng = pack_params(cfg, inp)
    ident = np.eye(128, dtype=np.float32)
    shared = {
        "w_in": np.asarray(inp["w_in"], np.float32), "w_gate_r": np.asarray(inp["w_gate_r"], np.float32),
        "w_gate_i": np.asarray(inp["w_gate_i"], np.float32), "w_out": np.asarray(inp["w_out"], np.float32),
        "w_ffn_gate": np.asarray(inp["w_ffn_gate"], np.float32), "w_ffn_up": np.asarray(inp["w_ffn_up"], np.float32),
        "w_ffn_down": np.asarray(inp["w_ffn_down"], np.float32), "prm": prm, "fng": fng, "ident": ident,
    }
    xs = np.asarray(inp["x"], np.float32)
    in_maps = [dict(shared, x=np.ascontiguousarray(xs[i])) for i in range(n_cores)]
    res = run_bass_kernel_spmd(nc, in_maps, core_ids=list(range(n_cores)), trace=trace)
    outs = np.stack([r["out"] for r in res.results], axis=0)
    return outs, res, stats


def kernel(**inputs):
    cfg = Cfg()
    outs, _, _ = run(cfg, inputs, 8)
    return outs.astype(np.float32)
```
